# Optimizing a Trainium2 kernel written in Bass

```python
import math
import jax, jax.numpy as jnp
from jax import lax
import numpy as np

D_MODEL = 1024
BATCH = 1
SEQ = 16384
DEPTH = 4

N_MIXERS = 4
RMS_EPS = 1e-6
D_FF = 4 * D_MODEL

CHUNK = 128
GMLP_HEADS = 8
GMLP_DIM = D_MODEL
GMLP_HEAD_DIM = GMLP_DIM // GMLP_HEADS

POOL_WINDOWS = (2, 4, 8, 16)
POOL_GROUP = D_MODEL // len(POOL_WINDOWS)

MLA_HEADS = 8
Q_LORA = 384
KV_LORA = 256
QK_NOPE = 128
QK_ROPE = 64
V_HEAD = 128
ROPE_THETA = 10000.0
Q_BLOCK = 128

S5_GROUP_CH = 16
S5_GROUPS = D_MODEL // S5_GROUP_CH
S5_STATE = 64
DT_MIN = 1e-3
DT_MAX = 1e-1

N_A = (DEPTH + 3) // 4
N_B = (DEPTH + 2) // 4
N_C = (DEPTH + 1) // 4
N_D = DEPTH // 4

kernel_name = "hybrid_interleaved_gmlp_pool_mla_s5"


def rmsnorm(x, g):
    xf = x.astype(jnp.float32)
    y = xf * lax.rsqrt(jnp.mean(xf * xf, axis=-1, keepdims=True) + RMS_EPS)
    return (y * g.astype(jnp.float32)).astype(x.dtype)


def layernorm(x, g, b):
    xf = x.astype(jnp.float32)
    mu = jnp.mean(xf, axis=-1, keepdims=True)
    xc = xf - mu
    y = xc * lax.rsqrt(jnp.mean(xc * xc, axis=-1, keepdims=True) + RMS_EPS)
    return (y * g.astype(jnp.float32) + b.astype(jnp.float32)).astype(x.dtype)


def gmlp_mixer(h, w_in, b_in, g_v, b_v, w_s, b_s, w_out):
    B_, S, _ = h.shape
    uv = jax.nn.gelu(h @ w_in + b_in)
    u, v = jnp.split(uv, 2, axis=-1)
    v = layernorm(v, g_v, b_v)
    v = v.reshape(B_, S // CHUNK, CHUNK, GMLP_HEADS, GMLP_HEAD_DIM)
    mask = jnp.tril(jnp.ones((CHUNK, CHUNK), dtype=bool))
    ws = jnp.where(mask[None], w_s, jnp.zeros_like(w_s))
    sv = jnp.einsum('hts,bcshd->bcthd', ws, v) + b_s.T[None, None, :, :, None]
    y = u * sv.reshape(B_, S, GMLP_DIM)
    return y @ w_out


def pool_mixer(h, w_grp, scale):
    B_, S, _ = h.shape
    hf = h.astype(jnp.float32)
    cs_all = jnp.cumsum(hf, axis=1)
    count = jnp.arange(1, S + 1, dtype=jnp.float32)[None, :, None]
    outs = []
    for g, w in enumerate(POOL_WINDOWS):
        sl = slice(g * POOL_GROUP, (g + 1) * POOL_GROUP)
        xg = hf[..., sl]
        cs = cs_all[..., sl]
        cs_lag = jnp.pad(cs, ((0, 0), (w, 0), (0, 0)))[:, :S]
        mean = (cs - cs_lag) / jnp.minimum(count, float(w))
        outs.append(jnp.einsum('bsc,cd->bsd', (mean - xg).astype(h.dtype), w_grp[g]))
    return jnp.concatenate(outs, axis=-1) * scale


def rope(x, cos, sin):
    x1, x2 = jnp.split(x, 2, axis=-1)
    return jnp.concatenate([x1 * cos - x2 * sin, x1 * sin + x2 * cos], axis=-1)


def mla_mixer(h, positions, w_dq, g_q, w_uq, w_dkv, g_kv, w_uk, w_uv, w_o):
    B_, S, _ = h.shape
    inv_freq = ROPE_THETA ** (-jnp.arange(0, QK_ROPE, 2, dtype=jnp.float32) / QK_ROPE)
    ang = positions.astype(jnp.float32)[..., None] * inv_freq
    cos = jnp.cos(ang).astype(h.dtype)
    sin = jnp.sin(ang).astype(h.dtype)

    q = (rmsnorm(h @ w_dq, g_q) @ w_uq).reshape(B_, S, MLA_HEADS, QK_NOPE + QK_ROPE)
    q_nope, q_rope = q[..., :QK_NOPE], q[..., QK_NOPE:]
    q_rope = rope(q_rope, cos[:, :, None, :], sin[:, :, None, :])

    ckv = h @ w_dkv
    c = rmsnorm(ckv[..., :KV_LORA], g_kv)
    k_rope = rope(ckv[..., KV_LORA:], cos, sin)
    k_nope = (c @ w_uk).reshape(B_, S, MLA_HEADS, QK_NOPE)
    v = (c @ w_uv).reshape(B_, S, MLA_HEADS, V_HEAD)

    scale = (QK_NOPE + QK_ROPE) ** -0.5
    nb = S // Q_BLOCK
    qn_b = q_nope.reshape(B_, nb, Q_BLOCK, MLA_HEADS, QK_NOPE).transpose(1, 0, 2, 3, 4)
    qr_b = q_rope.reshape(B_, nb, Q_BLOCK, MLA_HEADS, QK_ROPE).transpose(1, 0, 2, 3, 4)
    k_pos = jnp.arange(S)

    def block(args):
        i, qn, qr = args
        s = (jnp.einsum('bqhd,bkhd->bhqk', qn, k_nope)
             + jnp.einsum('bqhd,bkd->bhqk', qr, k_rope)).astype(jnp.float32) * scale
        q_pos = i * Q_BLOCK + jnp.arange(Q_BLOCK)
        s = jnp.where(k_pos[None, :] <= q_pos[:, None], s, jnp.finfo(jnp.float32).min)
        p = jax.nn.softmax(s, axis=-1).astype(v.dtype)
        return jnp.einsum('bhqk,bkhd->bqhd', p, v)

    o = lax.map(block, (jnp.arange(nb), qn_b, qr_b))
    o = o.transpose(1, 0, 2, 3, 4).reshape(B_, S, MLA_HEADS * V_HEAD)
    return o @ w_o


def s5_mixer(h, lam_re, lam_im, log_dt, b_re, b_im, c_re, c_im, d_skip, w_glu_a, w_glu_b):
    B_, S, _ = h.shape
    u = h.astype(jnp.float32).reshape(B_, S, S5_GROUPS, S5_GROUP_CH)
    dt = jnp.exp(log_dt.astype(jnp.float32))[:, None]
    lr = lam_re.astype(jnp.float32)
    li = lam_im.astype(jnp.float32)
    mag = jnp.exp(lr * dt)
    ab_re = mag * jnp.cos(li * dt)
    ab_im = mag * jnp.sin(li * dt)
    den = lr * lr + li * li
    f_re = ((ab_re - 1.0) * lr + ab_im * li) / den
    f_im = (ab_im * lr - (ab_re - 1.0) * li) / den
    br = b_re.astype(jnp.float32)
    bi = b_im.astype(jnp.float32)
    bb_re = f_re[..., None] * br - f_im[..., None] * bi
    bb_im = f_re[..., None] * bi + f_im[..., None] * br
    bu_re = jnp.einsum('bsgc,gpc->bsgp', u, bb_re)
    bu_im = jnp.einsum('bsgc,gpc->bsgp', u, bb_im)
    a_re = jnp.broadcast_to(ab_re, bu_re.shape)
    a_im = jnp.broadcast_to(ab_im, bu_im.shape)

    def combine(e1, e2):
        a1r, a1i, b1r, b1i = e1
        a2r, a2i, b2r, b2i = e2
        return (a2r * a1r - a2i * a1i,
                a2r * a1i + a2i * a1r,
                a2r * b1r - a2i * b1i + b2r,
                a2r * b1i + a2i * b1r + b2i)

    _, _, xr, xi = lax.associative_scan(combine, (a_re, a_im, bu_re, bu_im), axis=1)
    y = (jnp.einsum('bsgp,gcp->bsgc', xr, c_re.astype(jnp.float32))
         - jnp.einsum('bsgp,gcp->bsgc', xi, c_im.astype(jnp.float32)))
    y = y + d_skip.astype(jnp.float32) * u
    y = jax.nn.gelu(y.reshape(B_, S, D_MODEL)).astype(h.dtype)
    return (y @ w_glu_a) * jax.nn.sigmoid(y @ w_glu_b)


def setup_inputs(seed: int = 0) -> dict:
    key = jax.random.key(seed)
    ks = iter(jax.random.split(key, 40))
    nrm = lambda shape, s: jax.random.normal(next(ks), shape, jnp.float32) * s
    d = D_MODEL
    inp = {}
    inp["x"] = nrm((BATCH, SEQ, d), 1.0)
    inp["positions"] = jnp.broadcast_to(jnp.arange(SEQ, dtype=jnp.int32), (BATCH, SEQ))
    inp["norm_g"] = 1.0 + nrm((DEPTH, 4, d), 0.02)
    inp["w_up"] = nrm((DEPTH, d, D_FF), d ** -0.5)
    inp["w_down"] = nrm((DEPTH, D_FF, d), D_FF ** -0.5)
    inp["a_w_in"] = nrm((N_A, d, 2 * GMLP_DIM), d ** -0.5)
    inp["a_b_in"] = nrm((N_A, 2 * GMLP_DIM), 0.01)
    inp["a_g_v"] = 1.0 + nrm((N_A, GMLP_DIM), 0.02)
    inp["a_b_v"] = nrm((N_A, GMLP_DIM), 0.01)
    inp["a_w_s"] = nrm((N_A, GMLP_HEADS, CHUNK, CHUNK), CHUNK ** -0.5)
    inp["a_b_s"] = 1.0 + nrm((N_A, GMLP_HEADS, CHUNK), 0.01)
    inp["a_w_out"] = nrm((N_A, GMLP_DIM, d), GMLP_DIM ** -0.5)
    inp["b_w_grp"] = nrm((N_B, len(POOL_WINDOWS), POOL_GROUP, POOL_GROUP), POOL_GROUP ** -0.5)
    inp["b_scale"] = 1.0 + nrm((N_B, d), 0.1)
    inp["c_w_dq"] = nrm((N_C, d, Q_LORA), d ** -0.5)
    inp["c_g_q"] = 1.0 + nrm((N_C, Q_LORA), 0.02)
    inp["c_w_uq"] = nrm((N_C, Q_LORA, MLA_HEADS * (QK_NOPE + QK_ROPE)), Q_LORA ** -0.5)
    inp["c_w_dkv"] = nrm((N_C, d, KV_LORA + QK_ROPE), d ** -0.5)
    inp["c_g_kv"] = 1.0 + nrm((N_C, KV_LORA), 0.02)
    inp["c_w_uk"] = nrm((N_C, KV_LORA, MLA_HEADS * QK_NOPE), KV_LORA ** -0.5)
    inp["c_w_uv"] = nrm((N_C, KV_LORA, MLA_HEADS * V_HEAD), KV_LORA ** -0.5)
    inp["c_w_o"] = nrm((N_C, MLA_HEADS * V_HEAD, d), (MLA_HEADS * V_HEAD) ** -0.5)
    G, P, C = S5_GROUPS, S5_STATE, S5_GROUP_CH
    inp["d_lam_re"] = -0.5 + nrm((N_D, G, P), 0.01)
    inp["d_lam_im"] = jnp.pi * jnp.arange(P, dtype=jnp.float32) + nrm((N_D, G, P), 0.01)
    inp["d_log_dt"] = jax.random.uniform(next(ks), (N_D, G), jnp.float32,
                                          math.log(DT_MIN), math.log(DT_MAX))
    inp["d_b_re"] = nrm((N_D, G, P, C), (2 * C) ** -0.5)
    inp["d_b_im"] = nrm((N_D, G, P, C), (2 * C) ** -0.5)
    inp["d_c_re"] = nrm((N_D, G, C, P), (2 * P) ** -0.5)
    inp["d_c_im"] = nrm((N_D, G, C, P), (2 * P) ** -0.5)
    inp["d_skip"] = nrm((N_D, G, C), 1.0)
    inp["d_w_glu_a"] = nrm((N_D, d, d), d ** -0.5)
    inp["d_w_glu_b"] = nrm((N_D, d, d), d ** -0.5)
    return inp


def reference(x, positions, norm_g, w_up, w_down,
              a_w_in, a_b_in, a_g_v, a_b_v, a_w_s, a_b_s, a_w_out,
              b_w_grp, b_scale,
              c_w_dq, c_g_q, c_w_uq, c_w_dkv, c_g_kv, c_w_uk, c_w_uv, c_w_o,
              d_lam_re, d_lam_im, d_log_dt, d_b_re, d_b_im, d_c_re, d_c_im, d_skip,
              d_w_glu_a, d_w_glu_b):
    h = x
    for i in range(DEPTH):
        m, j = i % N_MIXERS, i // N_MIXERS
        z = rmsnorm(h, norm_g[i, 0])
        if m == 0:
            z = gmlp_mixer(z, a_w_in[j], a_b_in[j], a_g_v[j], a_b_v[j], a_w_s[j], a_b_s[j], a_w_out[j])
        elif m == 1:
            z = pool_mixer(z, b_w_grp[j], b_scale[j])
        elif m == 2:
            z = mla_mixer(z, positions, c_w_dq[j], c_g_q[j], c_w_uq[j], c_w_dkv[j], c_g_kv[j],
                          c_w_uk[j], c_w_uv[j], c_w_o[j])
        else:
            z = s5_mixer(z, d_lam_re[j], d_lam_im[j], d_log_dt[j], d_b_re[j], d_b_im[j],
                         d_c_re[j], d_c_im[j], d_skip[j], d_w_glu_a[j], d_w_glu_b[j])
        h = h + rmsnorm(z, norm_g[i, 1])
        z = rmsnorm(h, norm_g[i, 2])
        z = jnp.square(jax.nn.relu(z @ w_up[i])) @ w_down[i]
        h = h + rmsnorm(z, norm_g[i, 3])
    return h
```

```python
import numpy as np
import concourse.bass as bass
import concourse.mybir as mybir
from concourse.bass_utils import run_bass_kernel_spmd
from contextlib import ExitStack

F32 = mybir.dt.float32
BF16 = mybir.dt.bfloat16
I32 = mybir.dt.int32
AF = mybir.ActivationFunctionType
ALU = mybir.AluOpType
EPS = 1e-6
NCORES = 8
TWO_PI = float(2.0 * np.pi)
PI = float(np.pi)

ENGS = ("pe", "act", "dve", "pool", "sp")
SEM_LIMIT = 30000
DMA_POOL = 6


class Buf:
    __slots__ = ("ap", "w", "r")

    def __init__(self, ap):
        self.ap = ap
        self.w = None
        self.r = {}


class Prog:
    def __init__(self, nc, es):
        self.nc = nc
        self.es = es
        self.q = {e: [] for e in ENGS}
        self.csem = {}
        self.ccnt = {e: 0 for e in ENGS}
        self.dsem = {e: [] for e in ENGS}
        self.dcnt = {e: 0 for e in ENGS}
        self.waited = {e: {} for e in ENGS}
        self.nsem = 0
        self.sems = {}

    def _newsem(self):
        self.nsem += 1
        s = self.es.enter_context(self.nc.semaphore("s%d" % self.nsem))
        self.sems[id(s)] = s
        return s

    def _wait_list(self, eng, toks):
        need = {}
        for (s, v, src) in toks:
            if src == "pe" and eng == "pe":
                continue
            k = id(s)
            if self.waited[eng].get(k, 0) >= v:
                continue
            if need.get(k, 0) < v:
                need[k] = v
        out = []
        for k, v in need.items():
            self.waited[eng][k] = v
            out.append((self.sems[k], v))
        return out

    def op(self, eng, fn, reads=(), writes=(), dma=False):
        toks = []
        for b in reads:
            if b.w is not None:
                toks.append(b.w)
        for b in writes:
            if b.w is not None:
                toks.append(b.w)
            toks.extend(b.r.values())
        if dma:
            n = self.dcnt[eng]
            if len(self.dsem[eng]) < DMA_POOL:
                self.dsem[eng].append(self._newsem())
            s = self.dsem[eng][n % DMA_POOL]
            if n >= DMA_POOL:
                toks.append((s, 16 * (n // DMA_POOL), "dma"))
            val = 16 * (n // DMA_POOL + 1)
            self.dcnt[eng] = n + 1
            inc = 16
            tok = (s, val, "dma")
        else:
            if self.ccnt[eng] % SEM_LIMIT == 0:
                self.csem[eng] = self._newsem()
            self.ccnt[eng] += 1
            s = self.csem[eng]
            val = (self.ccnt[eng] - 1) % SEM_LIMIT + 1
            inc = 1
            tok = (s, val, eng)
        waits = self._wait_list(eng, toks)
        self.q[eng].append((waits, fn, s, inc))
        for b in writes:
            b.w = tok
            b.r = {}
        for b in reads:
            if b not in writes:
                b.r[(id(tok[0]))] = tok
        return tok

    def barrier(self):
        toks = []
        for e in ENGS:
            if self.ccnt[e] > 0:
                toks.append((self.csem[e], (self.ccnt[e] - 1) % SEM_LIMIT + 1, e))
            n = self.dcnt[e]
            for i, s in enumerate(self.dsem[e]):
                if n > i:
                    toks.append((s, 16 * ((n - i + DMA_POOL - 1) // DMA_POOL), "dma"))
        for e in ENGS:
            waits = self._wait_list(e, [t for t in toks if not (t[2] == e and e == "pe")])
            if waits:
                self.q[e].append((waits, None, None, 0))

    def wait_all(self, eng, bufs):
        toks = []
        for b in bufs:
            if b.w is not None:
                toks.append(b.w)
            toks.extend(b.r.values())
        waits = self._wait_list(eng, toks)
        self.q[eng].append((waits, None, None, 0))

    def emit(self):
        nc = self.nc
        with nc.Block() as block:
            def run(e, items):
                for (waits, fn, s, inc) in items:
                    for (ws, wv) in waits:
                        e.wait_ge(ws, wv)
                    if fn is not None:
                        fn(e).then_inc(s, inc)

            @block.tensor
            def _(e):
                run(e, self.q["pe"])

            @block.scalar
            def _(e):
                run(e, self.q["act"])

            @block.vector
            def _(e):
                run(e, self.q["dve"])

            @block.gpsimd
            def _(e):
                run(e, self.q["pool"])

            @block.sync
            def _(e):
                run(e, self.q["sp"])


class K:
    def __init__(self, nc, es):
        self.nc = nc
        self.es = es
        self.P = Prog(nc, es)
        self.nname = 0
        self.banks = [Buf(self.psum([128, 512], F32)) for _ in range(8)]
        self.bank_i = 0
        self.nrot = 8

    def name(self, p):
        self.nname += 1
        return "%s%d" % (p, self.nname)

    def sb(self, shape, dt, es=None):
        return (es or self.es).enter_context(self.nc.sbuf_tensor(self.name("t"), list(shape), dt))

    def psum(self, shape, dt):
        return self.es.enter_context(self.nc.psum_tensor(self.name("p"), list(shape), dt))

    def dram_in(self, name, shape, dt):
        return self.nc.dram_tensor(name, list(shape), dt, kind="ExternalInput").ap()

    def dram_out(self, name, shape, dt):
        return self.nc.dram_tensor(name, list(shape), dt, kind="ExternalOutput").ap()

    def bank(self):
        b = self.banks[self.bank_i % self.nrot]
        self.bank_i += 1
        return b

    def dma(self, out_b, out_ap, in_ap, in_b=None, eng="sp"):
        return self.P.op(eng, lambda e: e.dma_start(out=out_ap, in_=in_ap),
                         reads=([in_b] if in_b is not None else []), writes=[out_b], dma=True)

    def mm(self, out_b, out_ap, lhsT_ap, rhs_ap, start, stop, reads):
        return self.P.op("pe", lambda e: e.matmul(out_ap, lhsT=lhsT_ap, rhs=rhs_ap, start=start, stop=stop),
                         reads=reads, writes=[out_b])

    def act(self, out_b, out_ap, in_ap, func, reads, bias=None, scale=None, eng="act"):
        kw = {}
        if bias is not None:
            kw["bias"] = bias
        if scale is not None:
            kw["scale"] = scale
        return self.P.op("act", lambda e: e.activation(out=out_ap, in_=in_ap, func=func, **kw),
                         reads=reads, writes=[out_b])

    def tt(self, eng, out_b, out_ap, in0, in1, op, reads):
        return self.P.op(eng, lambda e: e.tensor_tensor(out=out_ap, in0=in0, in1=in1, op=op),
                         reads=reads, writes=[out_b])

    def ts(self, eng, out_b, out_ap, in0, s1, s2, op0, op1, reads):
        if op1 is None:
            return self.P.op(eng, lambda e: e.tensor_scalar(out=out_ap, in0=in0, scalar1=s1, scalar2=None, op0=op0),
                             reads=reads, writes=[out_b])
        return self.P.op(eng, lambda e: e.tensor_scalar(out=out_ap, in0=in0, scalar1=s1, scalar2=s2, op0=op0, op1=op1),
                         reads=reads, writes=[out_b])

    def stt(self, out_b, out_ap, in0, scalar, in1, op0, op1, reads):
        return self.P.op("dve", lambda e: e.scalar_tensor_tensor(out=out_ap, in0=in0, scalar=scalar, in1=in1, op0=op0, op1=op1),
                         reads=reads, writes=[out_b])

    def copy(self, eng, out_b, out_ap, in_ap, reads):
        if eng == "act":
            return self.P.op("act", lambda e: e.copy(out=out_ap, in_=in_ap), reads=reads, writes=[out_b])
        return self.P.op(eng, lambda e: e.tensor_copy(out=out_ap, in_=in_ap), reads=reads, writes=[out_b])

    def memset(self, eng, out_b, out_ap, val):
        return self.P.op(eng, lambda e: e.memset(out_ap, val), writes=[out_b])

    def finish(self, out_bufs):
        self.P.wait_all("sp", out_bufs)
        self.P.emit()


NT = 2048
HALO = 128
WIN = (2, 4, 8, 16)


class Common:
    def __init__(self, k, ntok, tiles):
        self.k = k
        self.ntok = ntok
        self.tiles = tiles
        nc = k.nc
        self.hT = k.sb([128, 8, ntok], F32)
        self.h_b = [[Buf(None) for _ in tiles] for _ in range(8)]
        self.ones = k.sb([128, 128], BF16)
        self.ones_b = Buf(None)
        k.memset("pool", self.ones_b, self.ones[:, :], 1.0)
        self.gcol = k.sb([128, 128], F32)
        self.gcol_b = Buf(None)
        self.sq = k.sb([128, 8, 512], BF16)
        self.sq_b = [Buf(None) for _ in range(8)]
        self.rstd = [k.sb([128, 512], F32) for _ in range(2)]
        self.rstd_b = [Buf(None) for _ in range(2)]
        self.ntmp = [k.sb([128, 512], F32) for _ in range(1)]
        self.ntmp_b = [Buf(None) for _ in range(2)]
        self.ncnt = 0
        self.epsc = k.sb([128, 1], F32)
        self.epsc_b = Buf(None)
        k.memset("pool", self.epsc_b, self.epsc[:, :], EPS)

    def load_gcol(self, gcol_dram):
        k = self.k
        k.dma(self.gcol_b, self.gcol[:, :], gcol_dram)

    def gc(self, l, n, kk):
        i = (l * 4 + n) * 8 + kk
        return self.gcol[:, i:i + 1]

    def norm(self, X, W, nfeat_chunks, out_fn, scale_pre, part=128):
        k = self.k
        nch = len(X)
        bank = k.bank()
        for c, (xb, xap) in enumerate(X):
            k.act(self.sq_b[c], self.sq[:part, c, :W], xap, AF.Square, [xb])
        for c in range(nch):
            k.mm(bank, bank.ap[:part, :W], self.ones[:part, :part], self.sq[:part, c, :W], c == 0, c == nch - 1,
                 [self.ones_b, self.sq_b[c]])
        i = self.ncnt % 2
        self.ncnt += 1
        k.act(self.rstd_b[i], self.rstd[i][:part, :W], bank.ap[:part, :W], AF.Sqrt, [bank, self.epsc_b], bias=self.epsc[:part, 0:1],
              scale=1.0 / float(nfeat_chunks))
        k.P.op("dve", lambda e, i=i: e.reciprocal(out=self.rstd[i][:part, :W], in_=self.rstd[i][:part, :W]),
               reads=[self.rstd_b[i]], writes=[self.rstd_b[i]])
        for c in range(nch):
            out_fn(c, self.rstd[i][:part, :W], self.rstd_b[i])

    def norm_to_z(self, l, n, ti, zT, z_b, src=None, src_b=None):
        k = self.k
        off, W = self.tiles[ti]
        X = [(self.h_b[c][ti], self.hT[:, c, off:off + W]) for c in range(8)]

        def out_fn(c, rs, rsb):
            k.stt(z_b[c][ti], zT[:, c, off:off + W], self.hT[:, c, off:off + W], self.gc(l, n, c), rs,
                  ALU.mult, ALU.mult, [self.h_b[c][ti], rsb, self.gcol_b])
        self.norm(X, W, 1024, out_fn, None)

    def norm_residual(self, l, n, ti, X):
        k = self.k
        off, W = self.tiles[ti]

        def out_fn(c, rs, rsb):
            j = 0
            k.stt(self.ntmp_b[j], self.ntmp[j][:, :W], X[c][1], self.gc(l, n, c), rs, ALU.mult, ALU.mult,
                  [X[c][0], rsb, self.gcol_b])
            k.tt("dve", self.h_b[c][ti], self.hT[:, c, off:off + W], self.hT[:, c, off:off + W], self.ntmp[j][:, :W],
                 ALU.add, [self.h_b[c][ti], self.ntmp_b[j]])
        self.norm(X, W, 1024, out_fn, None)


def ffn(cm, l, w_up, w_down, es, tile_ids=None):
    k = cm.k
    tiles = cm.tiles
    tids = list(range(len(tiles))) if tile_ids is None else tile_ids
    ntok = cm.ntok
    zT = k.sb([128, 8, ntok], BF16, es)
    z_b = [[Buf(None) for _ in tiles] for _ in range(8)]
    acc = k.sb([128, 8, ntok], F32, es)
    acc_b = [[Buf(None) for _ in tiles] for _ in range(8)]
    GH = 2
    NG = 32 // GH
    wup = [k.sb([128, 8, GH * 128], BF16, es) for _ in range(2)]
    wup_b = [Buf(None) for _ in range(2)]
    wdn = [k.sb([128, GH, 1024], BF16, es) for _ in range(2)]
    wdn_b = [Buf(None) for _ in range(2)]
    hid = [k.sb([128, GH, 512], BF16, es) for _ in range(2)]
    hid_b = [[Buf(None) for _ in range(GH)] for _ in range(2)]
    r32 = [k.sb([128, 512], F32, es) for _ in range(1)]
    r32_b = [Buf(None) for _ in range(1)]

    def load_group(g):
        i = g % 2
        k.dma(wup_b[i], wup[i][:, :, :],
              w_up[:, g * GH * 128:(g + 1) * GH * 128].rearrange("(k p) c -> p k c", p=128), eng="pool")
        k.dma(wdn_b[i], wdn[i][:, :, :],
              w_down[g * GH * 128:(g + 1) * GH * 128, :].rearrange("(j p) c -> p j c", p=128), eng="pool")

    load_group(0)
    load_group(1)
    for ti in tids:
        cm.norm_to_z(l, 2, ti, zT, z_b)

    steps = [(g, ti) for g in range(NG) for ti in tids]
    rcnt = [0]

    def up(si):
        g, ti = steps[si]
        off, W = tiles[ti]
        par = si % 2
        for j in range(GH):
            bank = k.bank()
            for kk in range(8):
                k.mm(bank, bank.ap[:, :W], wup[g % 2][:, kk, j * 128:(j + 1) * 128], zT[:, kk, off:off + W],
                     kk == 0, kk == 7, [wup_b[g % 2], z_b[kk][ti]])
            ri = 0
            rcnt[0] += 1
            k.act(r32_b[ri], r32[ri][:, :W], bank.ap[:, :W], AF.Relu, [bank])
            k.act(hid_b[par][j], hid[par][:, j, :W], r32[ri][:, :W], AF.Square, [r32_b[ri]])

    def down(si):
        g, ti = steps[si]
        off, W = tiles[ti]
        par = si % 2
        for m in range(8):
            bank = k.bank()
            for j in range(GH):
                k.mm(bank, bank.ap[:, :W], wdn[g % 2][:, j, m * 128:(m + 1) * 128], hid[par][:, j, :W],
                     j == 0, j == GH - 1, [wdn_b[g % 2], hid_b[par][j]])
            if g == 0:
                k.copy("dve", acc_b[m][ti], acc[:, m, off:off + W], bank.ap[:, :W], [bank])
            else:
                k.tt("dve", acc_b[m][ti], acc[:, m, off:off + W], acc[:, m, off:off + W], bank.ap[:, :W], ALU.add,
                     [bank, acc_b[m][ti]])

    n = len(steps)
    up(0)
    for si in range(n):
        if si + 1 < n:
            up(si + 1)
        down(si)
        g, ti = steps[si]
        if ti == tids[-1] and g + 2 < NG:
            load_group(g + 2)
    for ti in tids:
        off, W = tiles[ti]
        X = [(acc_b[m][ti], acc[:, m, off:off + W]) for m in range(8)]
        cm.norm_residual(l, 3, ti, X)


def gmlp(cm, l, d, es):
    k = cm.k
    tiles = cm.tiles
    win = k.sb([128, 8, 2048], BF16, es)
    win_b = Buf(None)
    k.dma(win_b, win[:, :, :], d["a_w_in"].rearrange("(k p) c -> p k c", p=128), eng="pool")
    wout = k.sb([128, 8, 1024], BF16, es)
    wout_b = Buf(None)
    k.dma(wout_b, wout[:, :, :], d["a_w_out"].rearrange("(k p) c -> p k c", p=128), eng="pool")
    wsf = k.sb([128, 8, 128], F32, es)
    wsf_b = Buf(None)
    k.dma(wsf_b, wsf[:, :, :], d["a_w_sT"].rearrange("h s t -> s h t"))
    mask = k.sb([128, 128], F32, es)
    mask_b = Buf(None)
    k.dma(mask_b, mask[:, :], d["mask"])
    wsT = k.sb([128, 8, 128], BF16, es)
    wsT_b = Buf(None)
    for hd in range(8):
        k.tt("dve", wsT_b, wsT[:, hd, :], wsf[:, hd, :], mask[:, :], ALU.mult, [wsf_b, mask_b])
    binu = k.sb([128, 8], F32, es)
    binu_b = Buf(None)
    k.dma(binu_b, binu[:, :], d["a_bin_u"])
    reps = {}
    for nm in ("a_bin_v", "a_gv", "a_bv", "a_bs"):
        t = k.sb([128, 1024], F32, es)
        b = Buf(None)
        k.dma(b, t[:, :], d[nm])
        reps[nm] = (t, b)
    zt = k.sb([128, 8, 512], BF16, es)
    zt_b = [[Buf(None)] for _ in range(8)]
    u = k.sb([128, 8, 512], BF16, es)
    u_b = [Buf(None) for _ in range(8)]
    v = k.sb([128, 1024], F32, es)
    v_b = [Buf(None) for _ in range(2)]
    vn = k.sb([128, 1024], BF16, es)
    vn_b = Buf(None)
    st = k.sb([128, 2, 6], F32, es)
    st_b = Buf(None)
    mv = k.sb([128, 4], F32, es)
    mv_b = Buf(None)
    y = k.sb([128, 8, 512], BF16, es)
    y_b = [Buf(None) for _ in range(8)]
    svt = k.sb([128, 8, 128], F32, es)
    svt_b = [Buf(None) for _ in range(8)]
    mix = k.sb([128, 8, 512], F32, es)
    mix_b = [Buf(None) for _ in range(8)]
    GELU = AF.Gelu_apprx_tanh

    for ti, (off, W) in enumerate(tiles):
        X = [(cm.h_b[c][ti], cm.hT[:, c, off:off + W]) for c in range(8)]

        def out_fn(c, rs, rsb, off=off, W=W, ti=ti):
            k.stt(zt_b[c][0], zt[:, c, :W], cm.hT[:, c, off:off + W], cm.gc(l, 0, c), rs, ALU.mult, ALU.mult,
                  [cm.h_b[c][ti], rsb, cm.gcol_b])
        cm.norm(X, W, 1024, out_fn, None)
        for j in range(8):
            bank = k.bank()
            for kk in range(8):
                k.mm(bank, bank.ap[:, :W], win[:, kk, j * 128:(j + 1) * 128], zt[:, kk, :W], kk == 0, kk == 7,
                     [win_b, zt_b[kk][0]])
            k.act(u_b[j], u[:, j, :W], bank.ap[:, :W], GELU, [bank, binu_b], bias=binu[:, j:j + 1])
        for c in range(W // 128):
            cs = slice(c * 128, (c + 1) * 128)
            for hf in range(2):
                fs = slice(hf * 512, (hf + 1) * 512)
                bank = k.bank()
                for kk in range(8):
                    k.mm(bank, bank.ap[:, :], zt[:, kk, cs], win[:, kk, 1024 + hf * 512:1024 + (hf + 1) * 512],
                         kk == 0, kk == 7, [win_b, zt_b[kk][0]])
                k.tt("dve", v_b[hf], v[:, fs], bank.ap[:, :], reps["a_bin_v"][0][:, fs], ALU.add,
                     [bank, reps["a_bin_v"][1]])
                k.act(v_b[hf], v[:, fs], v[:, fs], GELU, [v_b[hf]])
                k.P.op("dve", lambda e, hf=hf, fs=fs: e.bn_stats(out=st[:, hf, :], in_=v[:, fs]),
                       reads=[v_b[hf]], writes=[st_b])
            k.P.op("dve", lambda e: e.bn_aggr(out=mv[:, 0:2], in_=st[:, :, :]), reads=[st_b], writes=[mv_b])
            k.act(mv_b, mv[:, 2:3], mv[:, 1:2], AF.Sqrt, [mv_b, cm.epsc_b], bias=cm.epsc[:, 0:1])
            k.P.op("dve", lambda e: e.reciprocal(out=mv[:, 2:3], in_=mv[:, 2:3]), reads=[mv_b], writes=[mv_b])
            for hf in range(2):
                fs = slice(hf * 512, (hf + 1) * 512)
                k.ts("dve", v_b[hf], v[:, fs], v[:, fs], mv[:, 0:1], mv[:, 2:3], ALU.subtract, ALU.mult,
                     [v_b[hf], mv_b])
                k.tt("dve", v_b[hf], v[:, fs], v[:, fs], reps["a_gv"][0][:, fs], ALU.mult, [v_b[hf], reps["a_gv"][1]])
                k.tt("dve", vn_b, vn[:, fs], v[:, fs], reps["a_bv"][0][:, fs], ALU.add,
                     [v_b[hf], reps["a_bv"][1], vn_b])
            for hd in range(8):
                bank = k.bank()
                k.mm(bank, bank.ap[:, :128], vn[:, hd * 128:(hd + 1) * 128], wsT[:, hd, :], True, True, [vn_b, wsT_b])
                k.tt("dve", svt_b[hd], svt[:, hd, :], bank.ap[:, :128], reps["a_bs"][0][:, hd * 128:(hd + 1) * 128],
                     ALU.add, [bank, reps["a_bs"][1]])
                k.tt("dve", y_b[hd], y[:, hd, cs], svt[:, hd, :], u[:, hd, cs], ALU.mult, [svt_b[hd], u_b[hd], y_b[hd]])
        for m in range(8):
            bank = k.bank()
            for hd in range(8):
                k.mm(bank, bank.ap[:, :W], wout[:, hd, m * 128:(m + 1) * 128], y[:, hd, :W], hd == 0, hd == 7,
                     [wout_b, y_b[hd]])
            k.copy("act", mix_b[m], mix[:, m, :W], bank.ap[:, :W], [bank])
        X = [(mix_b[m], mix[:, m, :W]) for m in range(8)]
        cm.norm_residual(l, 1, ti, X)


def norm_rstd_only(cm, X, W, nfeat, out_b, out_ap):
    k = cm.k
    nch = len(X)
    bank = k.bank()
    for c, (xb, xap) in enumerate(X):
        k.act(cm.sq_b[c], cm.sq[:, c, :W], xap, AF.Square, [xb])
    for c in range(nch):
        k.mm(bank, bank.ap[:, :W], cm.ones[:, :], cm.sq[:, c, :W], c == 0, c == nch - 1, [cm.ones_b, cm.sq_b[c]])
    k.act(out_b, out_ap, bank.ap[:, :W], AF.Sqrt, [bank, cm.epsc_b, out_b], bias=cm.epsc[:, 0:1], scale=1.0 / float(nfeat))
    k.P.op("dve", lambda e: e.reciprocal(out=out_ap, in_=out_ap), reads=[out_b], writes=[out_b])


def pool_mixer(cm, l, d, es):
    k = cm.k
    tiles = cm.tiles
    NW = 16 + NT
    rs_all = k.sb([128, NW], F32, es)
    rs_b = Buf(None)
    rs0 = k.sb([128, 128], F32, es)
    rs0_b = Buf(None)
    A = k.sb([128, NW], F32, es)
    A_b = Buf(None)
    PB = [k.sb([128, NW], F32, es) for _ in range(2)]
    PB_b = [Buf(None) for _ in range(2)]
    dT = k.sb([128, 8, NT], BF16, es)
    d_b = [Buf(None) for _ in range(8)]
    wg = k.sb([128, 4, 2, 256], BF16, es)
    wg_b = Buf(None)
    for g in range(4):
        k.dma(wg_b, wg[:, g, :, :], d["b_w_grp"][g].rearrange("(kk p) c -> p kk c", p=128), eng="pool")
    bsc = k.sb([128, 8], F32, es)
    bsc_b = Buf(None)
    k.dma(bsc_b, bsc[:, :], d["b_scale"])
    hv = k.sb([128, 1], F32, es)
    hv_b = Buf(None)
    k.dma(hv_b, hv[:, :], d["halo_valid"])
    corr = k.sb([128, 4, 16], F32, es)
    corr_b = Buf(None)
    k.dma(corr_b, corr[:, :, :], d["pool_corr"])
    t16 = k.sb([128, 16], F32, es)
    t16_b = Buf(None)
    mix = k.sb([128, 8, 512], F32, es)
    mix_b = [Buf(None) for _ in range(8)]

    X0 = [(cm.h_b[c][0], cm.hT[:, c, 0:128]) for c in range(8)]
    norm_rstd_only(cm, X0, 128, 1024, rs0_b, rs0[:, :])
    k.copy("dve", rs_b, rs_all[:, 0:16], rs0[:, 112:128], [rs0_b])
    for ti in range(1, len(tiles)):
        off, W = tiles[ti]
        X = [(cm.h_b[c][ti], cm.hT[:, c, off:off + W]) for c in range(8)]
        tb = Buf(None)
        norm_rstd_only(cm, X, W, 1024, tb, rs_all[:, 16 + off - 128:16 + off - 128 + W])
        rs_b.w = tb.w
    allh = [cm.h_b[c][ti] for c in range(8) for ti in range(len(tiles))]
    for c in range(8):
        w = WIN[c // 2]
        hb = [cm.h_b[c][ti] for ti in range(len(tiles))]
        k.stt(A_b, A[:, :], cm.hT[:, c, 112:112 + NW], cm.gc(l, 0, c), rs_all[:, :], ALU.mult, ALU.mult,
              hb + [rs_b, cm.gcol_b])
        k.ts("dve", A_b, A[:, 0:16], A[:, 0:16], hv[:, 0:1], None, ALU.mult, None, [A_b, hv_b])
        S, S_b = A, A_b
        nst = {2: 1, 4: 2, 8: 3, 16: 4}[w]
        for s in range(nst):
            sh = 1 << s
            T, T_b = PB[s % 2], PB_b[s % 2]
            eng = "dve"
            k.tt(eng, T_b, T[:, sh:NW], S[:, sh:NW], S[:, 0:NW - sh], ALU.add, [S_b])
            S, S_b = T, T_b
        k.stt(d_b[c], dT[:, c, 16:NT], S[:, 32:NW], 1.0 / w, A[:, 32:NW], ALU.mult, ALU.subtract, [S_b, A_b])
        k.tt("dve", t16_b, t16[:, :], S[:, 16:32], corr[:, c // 2, :], ALU.mult, [S_b, corr_b])
        k.tt("dve", d_b[c], dT[:, c, 0:16], t16[:, :], A[:, 16:32], ALU.subtract, [t16_b, A_b, d_b[c]])
    for ti in range(1, len(tiles)):
        off, W = tiles[ti]
        o2 = off - 128
        for gi in range(4):
            for n in range(2):
                m = 2 * gi + n
                bank = k.bank()
                for kk in range(2):
                    k.mm(bank, bank.ap[:, :W], wg[:, gi, kk, n * 128:(n + 1) * 128], dT[:, 2 * gi + kk, o2:o2 + W],
                         kk == 0, kk == 1, [wg_b, d_b[2 * gi + kk]])
                k.ts("dve", mix_b[m], mix[:, m, :W], bank.ap[:, :W], bsc[:, m:m + 1], None, ALU.mult, None,
                     [bank, bsc_b])
        X = [(mix_b[m], mix[:, m, :W]) for m in range(8)]
        cm.norm_residual(l, 1, ti, X)


def rope_tables(k, es, pos_dram, invf_b, invf, W, cosb, cos_ap, sinb, sin_ap, scratch):
    posi, posi_b, q, q_b, ni, ni_b, nf, nf_b = scratch
    k.dma(posi_b, posi[:, :W], pos_dram)
    k.copy("dve", q_b, q[:, :W], posi[:, :W], [posi_b])
    k.ts("dve", q_b, q[:, :W], q[:, :W], invf[:, 0:1], None, ALU.mult, None, [q_b, invf_b])
    for (ob, oap, shift) in ((sinb, sin_ap, 0.0), (cosb, cos_ap, 0.25)):
        if shift != 0.0:
            k.ts("dve", q_b, q[:, :W], q[:, :W], shift, None, ALU.add, None, [q_b])
        k.copy("dve", ni_b, ni[:, :W], q[:, :W], [q_b])
        k.copy("dve", nf_b, nf[:, :W], ni[:, :W], [ni_b])
        k.tt("dve", nf_b, nf[:, :W], q[:, :W], nf[:, :W], ALU.subtract, [q_b, nf_b])
        k.act(ob, oap, nf[:, :W], AF.Sin, [nf_b, ob], scale=6.2831845)


def mla_pre(cm, l, d, es, lat_out, lat_b):
    k = cm.k
    tiles = cm.tiles
    zT = k.sb([128, 8, cm.ntok], BF16, es)
    z_b = [[Buf(None) for _ in tiles] for _ in range(8)]
    wdq = k.sb([128, 8, 384], BF16, es)
    wdq_b = Buf(None)
    k.dma(wdq_b, wdq[:, :, :], d["c_w_dq"].rearrange("(k p) c -> p k c", p=128), eng="pool")
    wdkv = k.sb([128, 8, 320], BF16, es)
    wdkv_b = Buf(None)
    k.dma(wdkv_b, wdkv[:, :, :], d["c_w_dkv"].rearrange("(k p) c -> p k c", p=128), eng="pool")
    gq = k.sb([128, 5], F32, es)
    gq_b = Buf(None)
    k.dma(gq_b, gq[:, :], d["c_gqkv"])
    invf = k.sb([64, 1], F32, es)
    invf_b = Buf(None)
    k.dma(invf_b, invf[:, :], d["invf"])
    sgn = k.sb([64, 1], F32, es)
    sgn_b = Buf(None)
    k.dma(sgn_b, sgn[:, :], d["sgn"])
    psw = k.sb([64, 64], F32, es)
    psw_b = Buf(None)
    k.dma(psw_b, psw[:, :], d["pswap"])
    cosT = k.sb([64, NT], F32, es)
    cos_b = Buf(None)
    sinT = k.sb([64, NT], F32, es)
    sin_b = Buf(None)
    scratch = (k.sb([64, NT], I32, es), Buf(None), k.sb([64, NT], F32, es), Buf(None),
               k.sb([64, NT], I32, es), Buf(None), k.sb([64, NT], F32, es), Buf(None))
    rope_tables(k, es, d["pos_rep"], invf_b, invf, NT, cos_b, cosT[:, :], sin_b, sinT[:, :], scratch)
    k.ts("dve", sin_b, sinT[:, :], sinT[:, :], sgn[:, 0:1], None, ALU.mult, None, [sin_b, sgn_b])
    raw = k.sb([128, 3, 512], F32, es)
    raw_b = [Buf(None) for _ in range(3)]
    outb = k.sb([128, 3, 512], BF16, es)
    outb_b = [Buf(None) for _ in range(3)]
    kr = k.sb([64, 512], F32, es)
    kr_b = Buf(None)
    t1 = k.sb([64, 512], F32, es)
    t1_b = Buf(None)
    t2 = k.sb([64, 512], F32, es)
    t2_b = Buf(None)
    kro = k.sb([64, 512], BF16, es)
    kro_b = Buf(None)
    for ti in range(1, len(tiles)):
        cm.norm_to_z(l, 0, ti, zT, z_b)
    for ti in range(1, len(tiles)):
        off, W = tiles[ti]
        o2 = off - 128
        for (wt, wt_b, nch, nfeat, gofs, rowofs) in ((wdq, wdq_b, 3, 384, 0, 0), (wdkv, wdkv_b, 2, 256, 3, 384)):
            for n in range(nch):
                bank = k.bank()
                for kk in range(8):
                    k.mm(bank, bank.ap[:, :W], wt[:, kk, n * 128:(n + 1) * 128], zT[:, kk, off:off + W], kk == 0, kk == 7,
                         [wt_b, z_b[kk][ti]])
                k.copy("act", raw_b[n], raw[:, n, :W], bank.ap[:, :W], [bank])
            X = [(raw_b[n], raw[:, n, :W]) for n in range(nch)]

            def out_fn(c, rs, rsb, W=W, gofs=gofs, rowofs=rowofs, o2=o2):
                k.stt(outb_b[c], outb[:, c, :W], raw[:, c, :W], gq[:, gofs + c:gofs + c + 1], rs, ALU.mult, ALU.mult,
                      [raw_b[c], rsb, gq_b])
                k.dma(lat_b, lat_out[rowofs + c * 128:rowofs + (c + 1) * 128, o2:o2 + W], outb[:, c, :W], in_b=outb_b[c])
            cm.norm(X, W, nfeat, out_fn, None)
        bank = k.bank()
        for kk in range(8):
            k.mm(bank, bank.ap[:64, :W], wdkv[:, kk, 256:320], zT[:, kk, off:off + W], kk == 0, kk == 7,
                 [wdkv_b, z_b[kk][ti]])
        k.copy("act", kr_b, kr[:, :W], bank.ap[:64, :W], [bank])
        bank2 = k.bank()
        k.mm(bank2, bank2.ap[:64, :W], psw[:, :], kr[:, :W], True, True, [psw_b, kr_b])
        k.tt("dve", t1_b, t1[:, :W], kr[:, :W], cosT[:, o2:o2 + W], ALU.mult, [kr_b, cos_b])
        k.tt("dve", t2_b, t2[:, :W], bank2.ap[:64, :W], sinT[:, o2:o2 + W], ALU.mult, [bank2, sin_b])
        k.tt("dve", kro_b, kro[:, :W], t1[:, :W], t2[:, :W], ALU.add, [t1_b, t2_b])
        k.dma(lat_b, lat_out[640:704, o2:o2 + W], kro[:, :W], in_b=kro_b)


S = 16384
SCALE = float((128 + 64) ** -0.5)


def attention(k, d, es, o_out, o_b):
    nc = k.nc
    k.nrot = 4
    oacc = [k.banks[4], k.banks[5]]
    lacc = [k.banks[6], k.banks[7]]
    ones = k.sb([128, 128], BF16, es)
    ones_b = Buf(None)
    k.memset("dve", ones_b, ones[:, :], 1.0)
    knT = k.sb([128, S], BF16, es)
    kn_b = [Buf(None) for _ in range(32)]
    krT = k.sb([65, S], BF16, es)
    kr_b = [Buf(None) for _ in range(32)]
    V = k.sb([128, 128, 128], BF16, es)
    v_b = [Buf(None) for _ in range(32)]
    wuq = k.sb([128, 3, 192], BF16, es)
    wuq_b = Buf(None)
    k.dma(wuq_b, wuq[:, :, :], d["w_uq"].rearrange("(k p) c -> p k c", p=128), eng="pool")
    wuk = k.sb([128, 2, 128], BF16, es)
    wuk_b = Buf(None)
    k.dma(wuk_b, wuk[:, :, :], d["w_uk"].rearrange("(k p) c -> p k c", p=128), eng="pool")
    wuv = k.sb([128, 2, 128], BF16, es)
    wuv_b = Buf(None)
    k.dma(wuv_b, wuv[:, :, :], d["w_uv"].rearrange("(k p) c -> p k c", p=128), eng="pool")
    tri = k.sb([128, 128], BF16, es)
    tri_b = Buf(None)
    k.dma(tri_b, tri[:, :], d["tri"], eng="pool")
    invf = k.sb([64, 1], F32, es)
    invf_b = Buf(None)
    k.dma(invf_b, invf[:, :], d["invf"])
    sgn = k.sb([64, 1], F32, es)
    sgn_b = Buf(None)
    k.dma(sgn_b, sgn[:, :], d["sgn"])
    psw = k.sb([64, 64], F32, es)
    psw_b = Buf(None)
    k.dma(psw_b, psw[:, :], d["pswap"])
    ct = [k.sb([128, 2, 512], BF16, es) for _ in range(2)]
    ct_b = [Buf(None) for _ in range(2)]
    sqs = k.sb([128, 2, 512], BF16, es)
    sqs_b = [Buf(None) for _ in range(2)]
    kmx = k.sb([65, 4], F32, es)
    kmx_b = Buf(None)
    k.memset("dve", kmx_b, kmx[:, :], 0.0)
    krone_b = Buf(None)
    k.memset("dve", krone_b, krT[64:65, :], 1.0)
    for t in range(32):
        ts_ = slice(t * 512, (t + 1) * 512)
        i = t % 2
        k.dma(ct_b[i], ct[i][:, :, :], d["latT"][384:640, ts_].rearrange("(k p) c -> p k c", p=128))
        k.dma(kr_b[t], krT[0:64, ts_], d["latT"][640:704, ts_])
        bank = k.bank()
        for kk in range(2):
            k.mm(bank, bank.ap[:, :], wuk[:, kk, :], ct[i][:, kk, :], kk == 0, kk == 1, [wuk_b, ct_b[i]])
        k.copy("act", kn_b[t], knT[:, ts_], bank.ap[:, :], [bank])
        for c4 in range(4):
            bank = k.bank()
            for kk in range(2):
                k.mm(bank, bank.ap[:, :128], ct[i][:, kk, c4 * 128:(c4 + 1) * 128], wuv[:, kk, :], kk == 0, kk == 1,
                     [wuv_b, ct_b[i]])
            k.copy("dve", v_b[t], V[:, t * 4 + c4, :], bank.ap[:, :128], [bank])
        k.act(sqs_b[0], sqs[:, 0, :], knT[:, ts_], AF.Square, [kn_b[t]])
        k.act(sqs_b[1], sqs[:64, 1, :], krT[0:64, ts_], AF.Square, [kr_b[t]])
        bank = k.bank()
        k.mm(bank, bank.ap[:65, :], ones[:, :65], sqs[:, 0, :], True, False, [ones_b, sqs_b[0]])
        k.mm(bank, bank.ap[:65, :], ones[:64, :65], sqs[:64, 1, :], False, True, [ones_b, sqs_b[1]])
        k.P.op("dve", lambda e, bank=bank: e.reduce_max(out=kmx[:, 1:2], in_=bank.ap[:65, :], axis=mybir.AxisListType.X),
               reads=[bank], writes=[kmx_b])
        k.tt("dve", kmx_b, kmx[:, 0:1], kmx[:, 0:1], kmx[:, 1:2], ALU.max, [kmx_b])
    k.act(kmx_b, kmx[:, 2:3], kmx[:, 0:1], AF.Sqrt, [kmx_b])
    k.ts("dve", kmx_b, kmx[:, 3:4], kmx[:, 2:3], -1.02, None, ALU.mult, None, [kmx_b])
    cq = [k.sb([128, 3, 512], BF16, es) for _ in range(2)]
    cq_b = [Buf(None) for _ in range(2)]
    qn = [k.sb([128, 512], BF16, es) for _ in range(2)]
    qn_b = [Buf(None) for _ in range(2)]
    qr = [k.sb([65, 512], BF16, es) for _ in range(2)]
    qr_b = [Buf(None) for _ in range(2)]
    qraw = k.sb([64, 512], F32, es)
    qraw_b = Buf(None)
    t1 = k.sb([64, 512], F32, es)
    t1_b = Buf(None)
    t2 = k.sb([64, 512], F32, es)
    t2_b = Buf(None)
    cosT = k.sb([64, 512], F32, es)
    cos_b = Buf(None)
    sinT = k.sb([64, 512], F32, es)
    sin_b = Buf(None)
    scratch = (k.sb([64, 512], I32, es), Buf(None), k.sb([64, 512], F32, es), Buf(None),
               k.sb([64, 512], I32, es), Buf(None), k.sb([64, 512], F32, es), Buf(None))
    qsq = k.sb([128, 2, 512], BF16, es)
    qsq_b = [Buf(None) for _ in range(2)]
    rrow = k.sb([65, 512], F32, es)
    rrow_b = Buf(None)
    NPT = 4
    PT = [k.sb([128, 512], BF16, es) for _ in range(NPT)]
    PT_b = [Buf(None) for _ in range(NPT)]
    rl = k.sb([128, 512], F32, es)
    rl_b = Buf(None)
    osb = [k.sb([128, 512], BF16, es) for _ in range(2)]
    osb_b = [Buf(None) for _ in range(2)]
    allk = kn_b + kr_b

    for qi in range(32):
        qs = slice(qi * 512, (qi + 1) * 512)
        i = qi % 2
        k.dma(cq_b[i], cq[i][:, :, :], d["latT"][0:384, qs].rearrange("(k p) c -> p k c", p=128))
        bank = k.bank()
        for kk in range(3):
            k.mm(bank, bank.ap[:, :], wuq[:, kk, 0:128], cq[i][:, kk, :], kk == 0, kk == 2, [wuq_b, cq_b[i]])
        k.copy("act", qn_b[i], qn[i][:, :], bank.ap[:, :], [bank])
        bank = k.bank()
        for kk in range(3):
            k.mm(bank, bank.ap[:64, :], wuq[:, kk, 128:192], cq[i][:, kk, :], kk == 0, kk == 2, [wuq_b, cq_b[i]])
        k.copy("act", qraw_b, qraw[:, :], bank.ap[:64, :], [bank])
        rope_tables(k, es, d["pos_rep"][:, qs], invf_b, invf, 512, cos_b, cosT[:, :], sin_b, sinT[:, :], scratch)
        k.ts("dve", sin_b, sinT[:, :], sinT[:, :], sgn[:, 0:1], None, ALU.mult, None, [sin_b, sgn_b])
        bank2 = k.bank()
        k.mm(bank2, bank2.ap[:64, :], psw[:, :], qraw[:, :], True, True, [psw_b, qraw_b])
        k.tt("dve", t1_b, t1[:, :], qraw[:, :], cosT[:, :], ALU.mult, [qraw_b, cos_b])
        k.tt("dve", t2_b, t2[:, :], bank2.ap[:64, :], sinT[:, :], ALU.mult, [bank2, sin_b])
        k.tt("dve", qr_b[i], qr[i][0:64, :], t1[:, :], t2[:, :], ALU.add, [t1_b, t2_b])
        k.act(qsq_b[0], qsq[:, 0, :], qn[i][:, :], AF.Square, [qn_b[i]])
        k.act(qsq_b[1], qsq[:64, 1, :], qr[i][0:64, :], AF.Square, [qr_b[i]])
        bank = k.bank()
        k.mm(bank, bank.ap[:65, :], ones[:, :65], qsq[:, 0, :], True, False, [ones_b, qsq_b[0]])
        k.mm(bank, bank.ap[:65, :], ones[:64, :65], qsq[:64, 1, :], False, True, [ones_b, qsq_b[1]])
        k.act(rrow_b, rrow[:, :], bank.ap[:65, :], AF.Sqrt, [bank])
        k.ts("dve", qr_b[i], qr[i][64:65, :], rrow[64:65, :], kmx[64:65, 3:4], None, ALU.mult, None,
             [rrow_b, kmx_b, qr_b[i]])
        nk = 4 * qi + 4
        oa, la = oacc[i], lacc[i]

        def qk(kc):
            j = kc - 4 * qi
            c0 = 128 * j if j > 0 else 0
            bank = k.bank()
            ks = slice(kc * 128, (kc + 1) * 128)
            k.mm(bank, bank.ap[:, c0:512], knT[:, ks], qn[i][:, c0:512], True, False, [kn_b[kc // 4], qn_b[i]])
            k.mm(bank, bank.ap[:, c0:512], krT[0:65, ks], qr[i][0:65, c0:512], False, True, [kr_b[kc // 4], qr_b[i], krone_b])
            p = kc % NPT
            k.act(PT_b[p], PT[p][:, c0:512], bank.ap[:, c0:512], AF.Exp, [bank], scale=SCALE)
            if j >= 0:
                k.tt("dve", PT_b[p], PT[p][:, c0:c0 + 128], PT[p][:, c0:c0 + 128], tri[:, :], ALU.mult, [PT_b[p], tri_b])

        def pv(kc):
            j = kc - 4 * qi
            c0 = 128 * j if j > 0 else 0
            p = kc % NPT
            k.mm(oa, oa.ap[:, c0:512], V[:, kc, :], PT[p][:, c0:512], kc == 0, kc == nk - 1, [v_b[kc // 4], PT_b[p]])
            k.mm(la, la.ap[:, c0:512], ones[:, :], PT[p][:, c0:512], kc == 0, kc == nk - 1, [ones_b, PT_b[p]])

        LOOK = 2
        for kc in range(min(LOOK, nk)):
            qk(kc)
        for kc in range(nk):
            if kc + LOOK < nk:
                qk(kc + LOOK)
            pv(kc)
        k.P.op("dve", lambda e, la=la: e.reciprocal(out=rl[:, :], in_=la.ap[:, :]), reads=[la], writes=[rl_b])
        k.tt("dve", osb_b[i], osb[i][:, :], oa.ap[:, :], rl[:, :], ALU.mult, [oa, rl_b])
        k.dma(o_b[qi], o_out[:, qs], osb[i][:, :], in_b=osb_b[i])


def load_h(cm, hin):
    k = cm.k
    for c in range(8):
        for ti, (off, W) in enumerate(cm.tiles):
            k.dma(cm.h_b[c][ti], cm.hT[:, c, off:off + W], hin[c * 128:(c + 1) * 128, off:off + W])


def store_h(cm, hout, ob):
    k = cm.k
    for c in range(8):
        for ti, (off, W) in enumerate(cm.tiles):
            b = Buf(None)
            k.dma(b, hout[c * 128:(c + 1) * 128, off:off + W], cm.hT[:, c, off:off + W], in_b=cm.h_b[c][ti])
            ob.append(b)


def wo_mixer(cm, l, d, es):
    k = cm.k
    oT = k.sb([128, 8, cm.ntok], BF16, es)
    o_b = [Buf(None) for _ in range(8)]
    for c in range(8):
        k.dma(o_b[c], oT[:, c, :], d["oT"][c * 128:(c + 1) * 128, :])
    wo = k.sb([128, 8, 1024], BF16, es)
    wo_b = Buf(None)
    k.dma(wo_b, wo[:, :, :], d["c_w_o"].rearrange("(k p) c -> p k c", p=128), eng="pool")
    mix = k.sb([128, 8, 512], F32, es)
    mix_b = [Buf(None) for _ in range(8)]
    for ti, (off, W) in enumerate(cm.tiles):
        for m in range(8):
            bank = k.bank()
            for kk in range(8):
                k.mm(bank, bank.ap[:, :W], wo[:, kk, m * 128:(m + 1) * 128], oT[:, kk, off:off + W], kk == 0, kk == 7,
                     [wo_b, o_b[kk]])
            k.copy("act", mix_b[m], mix[:, m, :W], bank.ap[:, :W], [bank])
        X = [(mix_b[m], mix[:, m, :W]) for m in range(8)]
        cm.norm_residual(l, 1, ti, X)


def z_out(cm, l, n, es, zdram, zb_list):
    k = cm.k
    zT = k.sb([128, 8, cm.ntok], BF16, es)
    z_b = [[Buf(None) for _ in cm.tiles] for _ in range(8)]
    for ti, (off, W) in enumerate(cm.tiles):
        cm.norm_to_z(l, n, ti, zT, z_b)
        for c in range(8):
            b = Buf(None)
            k.dma(b, zdram[c * 128:(c + 1) * 128, off:off + W], zT[:, c, off:off + W], in_b=z_b[c][ti])
            zb_list.append(b)


def glu_mixer(cm, l, d, es):
    k = cm.k
    yT = k.sb([128, 8, cm.ntok], BF16, es)
    y_b = [Buf(None) for _ in range(8)]
    for c in range(8):
        k.dma(y_b[c], yT[:, c, :], d["yT"][c * 128:(c + 1) * 128, :])
    wa = k.sb([128, 8, 1024], BF16, es)
    wa_b = Buf(None)
    k.dma(wa_b, wa[:, :, :], d["d_w_glu_a"].rearrange("(k p) c -> p k c", p=128), eng="pool")
    wb = k.sb([128, 8, 1024], BF16, es)
    wb_b = Buf(None)
    k.dma(wb_b, wb[:, :, :], d["d_w_glu_b"].rearrange("(k p) c -> p k c", p=128), eng="pool")
    sg = [k.sb([128, 512], F32, es) for _ in range(2)]
    sg_b = [Buf(None) for _ in range(2)]
    mix = k.sb([128, 8, 512], F32, es)
    mix_b = [Buf(None) for _ in range(8)]
    for ti, (off, W) in enumerate(cm.tiles):
        for m in range(8):
            ba = k.bank()
            for kk in range(8):
                k.mm(ba, ba.ap[:, :W], wa[:, kk, m * 128:(m + 1) * 128], yT[:, kk, off:off + W], kk == 0, kk == 7,
                     [wa_b, y_b[kk]])
            bb = k.bank()
            for kk in range(8):
                k.mm(bb, bb.ap[:, :W], wb[:, kk, m * 128:(m + 1) * 128], yT[:, kk, off:off + W], kk == 0, kk == 7,
                     [wb_b, y_b[kk]])
            i = m % 2
            k.act(sg_b[i], sg[i][:, :W], bb.ap[:, :W], AF.Sigmoid, [bb])
            k.tt("dve", mix_b[m], mix[:, m, :W], ba.ap[:, :W], sg[i][:, :W], ALU.mult, [ba, sg_b[i]])
        X = [(mix_b[m], mix[:, m, :W]) for m in range(8)]
        cm.norm_residual(l, 1, ti, X)


S = 16384
L = 512
INV2PI = float(1.0 / (2.0 * np.pi))
GELU = AF.Gelu_apprx_tanh


def sincos_turns(k, q, q_b, shape_ap, tmp_i, tmp_i_b, tmp_f, tmp_f_b, out_sin, out_sin_b, out_cos, out_cos_b):
    for (ob, oap, shift) in ((out_sin_b, out_sin, 0.0), (out_cos_b, out_cos, 0.25)):
        if shift != 0.0:
            k.ts("dve", q_b, q, q, shift, None, ALU.add, None, [q_b])
        k.copy("dve", tmp_i_b, tmp_i, q, [q_b])
        k.copy("dve", tmp_f_b, tmp_f, tmp_i, [tmp_i_b])
        k.tt("dve", tmp_f_b, tmp_f, q, tmp_f, ALU.subtract, [q_b, tmp_f_b])
        k.act(ob, oap, tmp_f, AF.Sin, [tmp_f_b, ob], scale=6.2831845)


def s5_scan(k, d, es, y_out, y_b):
    def T(shape, dt=F32):
        return k.sb(shape, dt, es), Buf(None)

    def load(name, shape, dt=F32, eng="sp"):
        t, b = T(shape, dt)
        k.dma(b, t[tuple(slice(None) for _ in shape)], d[name], eng=eng)
        return t, b
    lr, lr_b = load("lam_re_rep", [128, 512])
    li, li_b = load("lam_im_rep", [128, 512])
    ld, ld_b = load("logdt_rep", [128, 512])
    bre, bre_b = load("Bre_blk", [128, 512])
    bim, bim_b = load("Bim_blk", [128, 512])
    dt_, dt_b = T([128, 512])
    k.act(dt_b, dt_[:, :], ld[:, :], AF.Exp, [ld_b])
    lrdt, lrdt_b = T([128, 512])
    k.tt("dve", lrdt_b, lrdt[:, :], lr[:, :], dt_[:, :], ALU.mult, [lr_b, dt_b])
    q, q_b = T([128, 512])
    k.tt("dve", q_b, q[:, :], li[:, :], dt_[:, :], ALU.mult, [li_b, dt_b])
    k.ts("dve", q_b, q[:, :], q[:, :], INV2PI, None, ALU.mult, None, [q_b])
    mag, mag_b = T([128, 512])
    k.act(mag_b, mag[:, :], lrdt[:, :], AF.Exp, [lrdt_b])
    ti, ti_b = T([128, 512], I32)
    tf, tf_b = T([128, 512])
    sn, sn_b = T([128, 512])
    cs, cs_b = T([128, 512])
    sincos_turns(k, q[:, :], q_b, None, ti[:, :], ti_b, tf[:, :], tf_b, sn[:, :], sn_b, cs[:, :], cs_b)
    abre, abre_b = T([128, 512])
    abim, abim_b = T([128, 512])
    k.tt("dve", abre_b, abre[:, :], mag[:, :], cs[:, :], ALU.mult, [mag_b, cs_b])
    k.ts("dve", abre_b, abre[:, :], abre[:, :], -1.0, None, ALU.add, None, [abre_b])
    k.tt("dve", abim_b, abim[:, :], mag[:, :], sn[:, :], ALU.mult, [mag_b, sn_b])
    den, den_b = T([128, 512])
    t0, t0_b = T([128, 512])
    k.tt("dve", den_b, den[:, :], lr[:, :], lr[:, :], ALU.mult, [lr_b])
    k.tt("dve", t0_b, t0[:, :], li[:, :], li[:, :], ALU.mult, [li_b])
    k.tt("dve", den_b, den[:, :], den[:, :], t0[:, :], ALU.add, [den_b, t0_b])
    k.P.op("dve", lambda e: e.reciprocal(out=den[:, :], in_=den[:, :]), reads=[den_b], writes=[den_b])
    fre, fre_b = T([128, 512])
    fim, fim_b = T([128, 512])
    k.tt("dve", fre_b, fre[:, :], abre[:, :], lr[:, :], ALU.mult, [abre_b, lr_b])
    k.tt("dve", t0_b, t0[:, :], abim[:, :], li[:, :], ALU.mult, [abim_b, li_b, t0_b])
    k.tt("dve", fre_b, fre[:, :], fre[:, :], t0[:, :], ALU.add, [fre_b, t0_b])
    k.tt("dve", fre_b, fre[:, :], fre[:, :], den[:, :], ALU.mult, [fre_b, den_b])
    k.tt("dve", fim_b, fim[:, :], abim[:, :], lr[:, :], ALU.mult, [abim_b, lr_b])
    k.tt("dve", t0_b, t0[:, :], abre[:, :], li[:, :], ALU.mult, [abre_b, li_b, t0_b, fre_b])
    k.tt("dve", fim_b, fim[:, :], fim[:, :], t0[:, :], ALU.subtract, [fim_b, t0_b])
    k.tt("dve", fim_b, fim[:, :], fim[:, :], den[:, :], ALU.mult, [fim_b, den_b])
    BBre, BBre_b = T([128, 512], BF16)
    BBim, BBim_b = T([128, 512], BF16)
    t1, t1_b = T([128, 512])
    k.tt("dve", t0_b, t0[:, :], fre[:, :], bre[:, :], ALU.mult, [fre_b, bre_b, t0_b, fim_b])
    k.tt("dve", t1_b, t1[:, :], fim[:, :], bim[:, :], ALU.mult, [fim_b, bim_b])
    k.tt("dve", BBre_b, BBre[:, :], t0[:, :], t1[:, :], ALU.subtract, [t0_b, t1_b])
    k.tt("dve", t0_b, t0[:, :], fre[:, :], bim[:, :], ALU.mult, [fre_b, bim_b, t0_b, BBre_b])
    k.tt("dve", t1_b, t1[:, :], fim[:, :], bre[:, :], ALU.mult, [fim_b, bre_b, t1_b, BBre_b])
    k.tt("dve", BBim_b, BBim[:, :], t0[:, :], t1[:, :], ALU.add, [t0_b, t1_b])
    cre, cre_b = load("Cre_blk", [128, 512])
    cim, cim_b = load("Cim_blk", [128, 512])
    CCre, CCre_b = T([128, 512], BF16)
    CCim, CCim_b = T([128, 512], BF16)
    k.copy("dve", CCre_b, CCre[:, :], cre[:, :], [cre_b])
    k.ts("dve", CCim_b, CCim[:, :], cim[:, :], -1.0, None, ALU.mult, None, [cim_b])
    dsk, dsk_b = load("dskip_col", [128, 1])
    lrc, lrc_b = load("lam_re_col", [128, 4])
    lic, lic_b = load("lam_im_col", [128, 4])
    ldc, ldc_b = load("logdt_col", [128, 4])
    dtc, dtc_b = T([128, 4])
    k.act(dtc_b, dtc[:, :], ldc[:, :], AF.Exp, [ldc_b])
    rho, rho_b = T([128, 4])
    k.tt("dve", rho_b, rho[:, :], lrc[:, :], dtc[:, :], ALU.mult, [lrc_b, dtc_b])
    k.act(rho_b, rho[:, :], rho[:, :], AF.Exp, [rho_b])
    w2, w2_b = T([128, 4])
    k.tt("dve", w2_b, w2[:, :], lic[:, :], dtc[:, :], ALU.mult, [lic_b, dtc_b])
    k.ts("dve", w2_b, w2[:, :], w2[:, :], INV2PI, None, ALU.mult, None, [w2_b])
    qL, qL_b = T([128, 4])
    k.ts("dve", qL_b, qL[:, :], w2[:, :], float(L), None, ALU.mult, None, [w2_b])
    ti4, ti4_b = T([128, 4], I32)
    tf4, tf4_b = T([128, 4])
    sL, sL_b = T([128, 4])
    cL, cL_b = T([128, 4])
    sincos_turns(k, qL[:, :], qL_b, None, ti4[:, :], ti4_b, tf4[:, :], tf4_b, sL[:, :], sL_b, cL[:, :], cL_b)
    tau, tau_b = load("tau_rep", [128, L])
    cosT, cos_b = T([128, 4, L])
    sinT, sin_b = T([128, 4, L])
    rhoT, rhoT_b = T([128, 4, L])
    for j in range(4):
        k.ts("dve", q_b, q[:, :], tau[:, :], w2[:, j:j + 1], None, ALU.mult, None, [tau_b, w2_b, q_b])
        sincos_turns(k, q[:, :], q_b, None, ti[:, :], ti_b, tf[:, :], tf_b, sinT[:, j, :], sin_b, cosT[:, j, :], cos_b)
        k.ts("dve", rhoT_b, rhoT[:, j, :], tau[:, :], 0.0, rho[:, j:j + 1], ALU.mult, ALU.add, [tau_b, rho_b])
    uT = k.sb([128, S], BF16, es)
    u_b = [Buf(None) for _ in range(32)]
    ini, ini_b = T([128, 4, 2])
    k.memset("dve", ini_b, ini[:, :, :], 0.0)
    ini2, ini2_b = T([128, 4, 2])
    ur, ur_b = T([128, L])
    ui, ui_b = T([128, L])
    xr, xr_b = T([128, L])
    xi, xi_b = T([128, L])
    a1, a1_b = T([128, L])
    a2, a2_b = T([128, L])
    Xre = [k.sb([128, L], BF16, es) for _ in range(2)]
    Xre_b = [Buf(None) for _ in range(2)]
    Xim = [k.sb([128, L], BF16, es) for _ in range(2)]
    Xim_b = [Buf(None) for _ in range(2)]
    yf, yf_b = T([128, L])
    yo = [k.sb([128, L], BF16, es) for _ in range(2)]
    yo_b = [Buf(None) for _ in range(2)]
    k.nrot = 6
    ybank = [k.banks[6], k.banks[7]]
    for t in range(32):
        k.dma(u_b[t], uT[:, t * L:(t + 1) * L], d["uT"][:, t * L:(t + 1) * L])
    cnt = 0
    for t in range(32):
        tsl = slice(t * L, (t + 1) * L)
        yb = ybank[t % 2]
        for j in range(4):
            js = slice(j * 128, (j + 1) * 128)
            br = k.bank()
            k.mm(br, br.ap[:, :], BBre[:, js], uT[:, tsl], True, True, [BBre_b, u_b[t]])
            bi = k.bank()
            k.mm(bi, bi.ap[:, :], BBim[:, js], uT[:, tsl], True, True, [BBim_b, u_b[t]])
            k.tt("dve", a1_b, a1[:, :], br.ap[:, :], cosT[:, j, :], ALU.mult, [br, cos_b])
            k.tt("dve", a2_b, a2[:, :], bi.ap[:, :], sinT[:, j, :], ALU.mult, [bi, sin_b])
            k.tt("dve", ur_b, ur[:, :], a1[:, :], a2[:, :], ALU.add, [a1_b, a2_b])
            k.tt("dve", a1_b, a1[:, :], bi.ap[:, :], cosT[:, j, :], ALU.mult, [bi, cos_b, a1_b])
            k.tt("dve", a2_b, a2[:, :], br.ap[:, :], sinT[:, j, :], ALU.mult, [br, sin_b, a2_b])
            k.tt("dve", ui_b, ui[:, :], a1[:, :], a2[:, :], ALU.subtract, [a1_b, a2_b])
            k.P.op("dve", lambda e, j=j: e.tensor_tensor_scan(out=xr[:, :], data0=rhoT[:, j, :], data1=ur[:, :],
                                                             initial=ini[:, j, 0:1], op0=ALU.mult, op1=ALU.add),
                   reads=[rhoT_b, ur_b, ini_b], writes=[xr_b])
            k.P.op("dve", lambda e, j=j: e.tensor_tensor_scan(out=xi[:, :], data0=rhoT[:, j, :], data1=ui[:, :],
                                                             initial=ini[:, j, 1:2], op0=ALU.mult, op1=ALU.add),
                   reads=[rhoT_b, ui_b, ini_b], writes=[xi_b])
            k.tt("dve", ini2_b, ini2[:, j, 0:1], xr[:, L - 1:L], cL[:, j:j + 1], ALU.mult, [xr_b, cL_b])
            k.tt("dve", ini2_b, ini2[:, j, 1:2], xi[:, L - 1:L], sL[:, j:j + 1], ALU.mult, [xi_b, sL_b, ini2_b])
            k.tt("dve", ini_b, ini[:, j, 0:1], ini2[:, j, 0:1], ini2[:, j, 1:2], ALU.subtract, [ini2_b, ini_b])
            k.tt("dve", ini2_b, ini2[:, j, 0:1], xr[:, L - 1:L], sL[:, j:j + 1], ALU.mult, [xr_b, sL_b, ini2_b])
            k.tt("dve", ini2_b, ini2[:, j, 1:2], xi[:, L - 1:L], cL[:, j:j + 1], ALU.mult, [xi_b, cL_b, ini2_b])
            k.tt("dve", ini_b, ini[:, j, 1:2], ini2[:, j, 0:1], ini2[:, j, 1:2], ALU.add, [ini2_b, ini_b])
            p = cnt % 2
            cnt += 1
            k.tt("dve", a1_b, a1[:, :], xr[:, :], cosT[:, j, :], ALU.mult, [xr_b, cos_b, a1_b])
            k.tt("dve", a2_b, a2[:, :], xi[:, :], sinT[:, j, :], ALU.mult, [xi_b, sin_b, a2_b])
            k.tt("dve", Xre_b[p], Xre[p][:, :], a1[:, :], a2[:, :], ALU.subtract, [a1_b, a2_b])
            k.tt("dve", a1_b, a1[:, :], xi[:, :], cosT[:, j, :], ALU.mult, [xi_b, cos_b, a1_b])
            k.tt("dve", a2_b, a2[:, :], xr[:, :], sinT[:, j, :], ALU.mult, [xr_b, sin_b, a2_b])
            k.tt("dve", Xim_b[p], Xim[p][:, :], a1[:, :], a2[:, :], ALU.add, [a1_b, a2_b])
            k.mm(yb, yb.ap[:, :], CCre[:, js], Xre[p][:, :], j == 0, False, [CCre_b, Xre_b[p]])
            k.mm(yb, yb.ap[:, :], CCim[:, js], Xim[p][:, :], False, j == 3, [CCim_b, Xim_b[p]])
        k.stt(yf_b, yf[:, :], uT[:, tsl], dsk[:, 0:1], yb.ap[:, :], ALU.mult, ALU.add, [u_b[t], dsk_b, yb])
        o = t % 2
        k.act(yo_b[o], yo[o][:, :], yf[:, :], GELU, [yf_b])
        k.dma(y_b[t], y_out[:, tsl], yo[o][:, :], in_b=yo_b[o])

TILES0 = [(0, 128), (128, 512), (640, 512), (1152, 512), (1664, 512)]
TILES = [(0, 512), (512, 512), (1024, 512), (1536, 512)]
INV_FREQ = (10000.0 ** (-np.arange(0, 64, 2, dtype=np.float32) / 64)).astype(np.float32)


def build_A():
    nc = bass.Bass("TRN2", target_bir_lowering=False)
    with ExitStack() as es:
        k = K(nc, es)
        d = {}

        def di(n, s, dt=F32):
            d[n] = k.dram_in(n, s, dt)
        di("xT", [1024, 2176]); di("gcol", [128, 128]); di("a_w_in", [1024, 2048]); di("a_w_out", [1024, 1024])
        di("a_w_sT", [8, 128, 128]); di("mask", [128, 128]); di("a_bin_u", [128, 8])
        for nm in ("a_bin_v", "a_gv", "a_bv", "a_bs"):
            di(nm, [128, 1024])
        for l in range(2):
            di("w_up%d" % l, [1024, 4096]); di("w_down%d" % l, [4096, 1024])
        di("b_w_grp", [4, 256, 256]); di("b_scale", [128, 8]); di("halo_valid", [128, 1]); di("pool_corr", [128, 4, 16])
        di("c_w_dq", [1024, 384]); di("c_w_dkv", [1024, 320]); di("c_gqkv", [128, 5]); di("invf", [64, 1]); di("sgn", [64, 1])
        di("pswap", [64, 64]); di("pos_rep", [64, 2048], I32)
        hout = k.dram_out("hT_out", [1024, 2048], F32)
        lat = k.dram_out("latT", [704, 2048], BF16)
        cm = Common(k, 2176, TILES0)
        cm.load_gcol(d["gcol"])
        for c in range(8):
            for ti, (off, W) in enumerate(TILES0):
                k.dma(cm.h_b[c][ti], cm.hT[:, c, off:off + W], d["xT"][c * 128:(c + 1) * 128, off:off + W])
        k.P.barrier()
        with ExitStack() as es2:
            gmlp(cm, 0, d, es2)
        k.P.barrier()
        with ExitStack() as es2:
            ffn(cm, 0, d["w_up0"], d["w_down0"], es2)
        k.P.barrier()
        with ExitStack() as es2:
            pool_mixer(cm, 1, d, es2)
        k.P.barrier()
        with ExitStack() as es2:
            ffn(cm, 1, d["w_up1"], d["w_down1"], es2, tile_ids=[1, 2, 3, 4])
        k.P.barrier()
        outs = []
        for c in range(8):
            for ti, (off, W) in enumerate(TILES0):
                if ti == 0:
                    continue
                b = Buf(None)
                k.dma(b, hout[c * 128:(c + 1) * 128, off - 128:off - 128 + W], cm.hT[:, c, off:off + W],
                      in_b=cm.h_b[c][ti], eng="sp")
                outs.append(b)
        lat_b = Buf(None)
        with ExitStack() as es2:
            mla_pre(cm, 2, d, es2, lat, lat_b)
        k.finish(outs + [lat_b])
    return nc


def host_common(inp):
    g = inp["norm_g"]
    return np.ascontiguousarray(g.reshape(4, 4, 8, 128).transpose(3, 0, 1, 2).reshape(128, 128))


def rope_consts():
    psw = np.zeros((64, 64), np.float32)
    for m in range(64):
        psw[(m + 32) % 64, m] = 1.0
    invf = (np.concatenate([INV_FREQ, INV_FREQ]) / np.float32(2 * np.pi)).astype(np.float32).reshape(64, 1)
    sgn = np.concatenate([-np.ones(32), np.ones(32)]).astype(np.float32).reshape(64, 1)
    return psw, invf, sgn


def host_A(inp):
    x = inp["x"][0]
    xT = np.ascontiguousarray(x.T)
    gcol = host_common(inp)
    mask = (np.arange(128)[:, None] <= np.arange(128)[None, :]).astype(np.float32)
    psw, invf, sgn = rope_consts()
    common = dict(
        gcol=gcol, a_w_in=inp["a_w_in"][0], a_w_out=inp["a_w_out"][0],
        a_w_sT=np.ascontiguousarray(inp["a_w_s"][0].transpose(0, 2, 1)), mask=mask,
        a_bin_u=np.ascontiguousarray(inp["a_b_in"][0][:1024].reshape(8, 128).T),
        a_bin_v=np.ascontiguousarray(np.broadcast_to(inp["a_b_in"][0][1024:], (128, 1024))),
        a_gv=np.ascontiguousarray(np.broadcast_to(inp["a_g_v"][0], (128, 1024))),
        a_bv=np.ascontiguousarray(np.broadcast_to(inp["a_b_v"][0], (128, 1024))),
        a_bs=np.ascontiguousarray(np.broadcast_to(inp["a_b_s"][0].reshape(1024), (128, 1024))),
        w_up0=inp["w_up"][0], w_down0=inp["w_down"][0], w_up1=inp["w_up"][1], w_down1=inp["w_down"][1],
        b_w_grp=inp["b_w_grp"][0], b_scale=np.ascontiguousarray(inp["b_scale"][0].reshape(8, 128).T),
        c_w_dq=inp["c_w_dq"][0], c_w_dkv=inp["c_w_dkv"][0],
        c_gqkv=np.ascontiguousarray(np.concatenate([inp["c_g_q"][0], inp["c_g_kv"][0]]).reshape(5, 128).T),
        invf=invf, sgn=sgn, pswap=psw,
    )
    maps = []
    for c in range(8):
        xt = np.zeros((1024, 2176), np.float32)
        lo = c * 2048 - 128
        if c == 0:
            xt[:, 128:] = xT[:, 0:2048]
        else:
            xt[:] = xT[:, lo:lo + 2176]
        corr = np.zeros((128, 4, 16), np.float32)
        for gi, w in enumerate((2, 4, 8, 16)):
            if c == 0:
                corr[:, gi, :] = 1.0 / np.minimum(np.arange(1, 17), w).astype(np.float32)
            else:
                corr[:, gi, :] = 1.0 / w
        m = dict(common)
        m["xT"] = xt
        m["halo_valid"] = np.full((128, 1), 0.0 if c == 0 else 1.0, np.float32)
        m["pool_corr"] = corr
        m["pos_rep"] = np.ascontiguousarray(
            np.broadcast_to(inp["positions"][0][c * 2048:(c + 1) * 2048], (64, 2048))).astype(np.int32)
        maps.append(m)
    return maps


def build_B():
    nc = bass.Bass("TRN2", target_bir_lowering=False)
    with ExitStack() as es:
        k = K(nc, es)
        d = {}

        def di(n, s, dt=F32):
            d[n] = k.dram_in(n, s, dt)
        di("latT", [704, 16384], BF16); di("w_uq", [384, 192]); di("w_uk", [256, 128]); di("w_uv", [256, 128])
        di("tri", [128, 128]); di("invf", [64, 1]); di("sgn", [64, 1]); di("pswap", [64, 64]); di("pos_rep", [64, 16384], I32)
        o_out = k.dram_out("oT", [128, 16384], BF16)
        o_b = [Buf(None) for _ in range(32)]
        with ExitStack() as es2:
            attention(k, d, es2, o_out, o_b)
        k.finish(o_b)
    return nc


def host_B(inp, latT):
    psw, invf, sgn = rope_consts()
    tri = (np.arange(128)[:, None] <= np.arange(128)[None, :]).astype(np.float32)
    pos_rep = np.ascontiguousarray(np.broadcast_to(inp["positions"][0], (64, 16384))).astype(np.int32)
    maps = []
    for h in range(8):
        maps.append(dict(latT=latT, w_uq=np.ascontiguousarray(inp["c_w_uq"][0][:, h * 192:(h + 1) * 192]),
                         w_uk=np.ascontiguousarray(inp["c_w_uk"][0][:, h * 128:(h + 1) * 128]),
                         w_uv=np.ascontiguousarray(inp["c_w_uv"][0][:, h * 128:(h + 1) * 128]),
                         tri=tri, invf=invf, sgn=sgn, pswap=psw, pos_rep=pos_rep))
    return maps


def build_C():
    nc = bass.Bass("TRN2", target_bir_lowering=False)
    with ExitStack() as es:
        k = K(nc, es)
        d = {}

        def di(n, s, dt=F32):
            d[n] = k.dram_in(n, s, dt)
        di("hT_in", [1024, 2048]); di("gcol", [128, 128]); di("oT", [1024, 2048], BF16); di("c_w_o", [1024, 1024])
        di("w_up", [1024, 4096]); di("w_down", [4096, 1024])
        hout = k.dram_out("hT_out", [1024, 2048], F32)
        zout = k.dram_out("zT_out", [1024, 2048], BF16)
        cm = Common(k, 2048, TILES)
        cm.load_gcol(d["gcol"])
        load_h(cm, d["hT_in"])
        k.P.barrier()
        with ExitStack() as es2:
            wo_mixer(cm, 2, d, es2)
        k.P.barrier()
        with ExitStack() as es2:
            ffn(cm, 2, d["w_up"], d["w_down"], es2)
        k.P.barrier()
        outs = []
        store_h(cm, hout, outs)
        with ExitStack() as es2:
            z_out(cm, 3, 0, es2, zout, outs)
        k.finish(outs)
    return nc


def build_D():
    nc = bass.Bass("TRN2", target_bir_lowering=False)
    with ExitStack() as es:
        k = K(nc, es)
        d = {}

        def di(n, s, dt=F32):
            d[n] = k.dram_in(n, s, dt)
        di("uT", [128, 16384], BF16)
        for n in ("lam_re_rep", "lam_im_rep", "logdt_rep", "Bre_blk", "Bim_blk", "Cre_blk", "Cim_blk", "tau_rep"):
            di(n, [128, 512])
        for n in ("lam_re_col", "lam_im_col", "logdt_col"):
            di(n, [128, 4])
        di("dskip_col", [128, 1])
        y_out = k.dram_out("yT", [128, 16384], BF16)
        y_b = [Buf(None) for _ in range(32)]
        with ExitStack() as es2:
            s5_scan(k, d, es2, y_out, y_b)
        k.finish(y_b)
    return nc


def host_D(inp, z3T):
    maps = []
    lre, lim, ldt = inp["d_lam_re"][0], inp["d_lam_im"][0], inp["d_log_dt"][0]
    bre, bim, cre, cim, dsk = inp["d_b_re"][0], inp["d_b_im"][0], inp["d_c_re"][0], inp["d_c_im"][0], inp["d_skip"][0]
    tau = np.ascontiguousarray(np.broadcast_to(np.arange(512, dtype=np.float32), (128, 512)))
    ldt_full = np.repeat(ldt[:, None], 64, axis=1)
    for i in range(8):
        G0 = 8 * i

        def col(a):
            o = np.zeros((128, 4), np.float32)
            for j in range(4):
                for gg in range(2):
                    o[gg * 64:(gg + 1) * 64, j] = a[G0 + 2 * j + gg]
            return o

        def rep(a):
            v = np.concatenate([a[G0 + gl] for gl in range(8)])
            return np.ascontiguousarray(np.broadcast_to(v, (128, 512))).astype(np.float32)
        Bre = np.zeros((128, 512), np.float32)
        Bim = np.zeros((128, 512), np.float32)
        Cre = np.zeros((128, 4, 128), np.float32)
        Cim = np.zeros((128, 4, 128), np.float32)
        for gl in range(8):
            j, gg = gl // 2, gl % 2
            Bre[gl * 16:(gl + 1) * 16, j * 128 + gg * 64:j * 128 + (gg + 1) * 64] = bre[G0 + gl].T
            Bim[gl * 16:(gl + 1) * 16, j * 128 + gg * 64:j * 128 + (gg + 1) * 64] = bim[G0 + gl].T
            Cre[gg * 64:(gg + 1) * 64, j, gl * 16:(gl + 1) * 16] = cre[G0 + gl].T
            Cim[gg * 64:(gg + 1) * 64, j, gl * 16:(gl + 1) * 16] = cim[G0 + gl].T
        maps.append(dict(uT=np.ascontiguousarray(z3T[i * 128:(i + 1) * 128]),
                         lam_re_rep=rep(lre), lam_im_rep=rep(lim), logdt_rep=rep(ldt_full),
                         Bre_blk=Bre, Bim_blk=Bim, Cre_blk=Cre.reshape(128, 512), Cim_blk=Cim.reshape(128, 512),
                         tau_rep=tau, lam_re_col=col(lre), lam_im_col=col(lim), logdt_col=col(ldt_full),
                         dskip_col=np.ascontiguousarray(dsk[G0:G0 + 8].reshape(128, 1))))
    return maps


def build_E():
    nc = bass.Bass("TRN2", target_bir_lowering=False)
    with ExitStack() as es:
        k = K(nc, es)
        d = {}

        def di(n, s, dt=F32):
            d[n] = k.dram_in(n, s, dt)
        di("hT_in", [1024, 2048]); di("gcol", [128, 128]); di("yT", [1024, 2048], BF16)
        di("d_w_glu_a", [1024, 1024]); di("d_w_glu_b", [1024, 1024])
        di("w_up", [1024, 4096]); di("w_down", [4096, 1024])
        hout = k.dram_out("hT_out", [1024, 2048], F32)
        cm = Common(k, 2048, TILES)
        cm.load_gcol(d["gcol"])
        load_h(cm, d["hT_in"])
        k.P.barrier()
        with ExitStack() as es2:
            glu_mixer(cm, 3, d, es2)
        k.P.barrier()
        with ExitStack() as es2:
            ffn(cm, 3, d["w_up"], d["w_down"], es2)
        k.P.barrier()
        outs = []
        store_h(cm, hout, outs)
        k.finish(outs)
    return nc


_CACHE = {}


def get_nc(name, fn):
    if name not in _CACHE:
        _CACHE[name] = fn()
    return _CACHE[name]


def run(nc, maps):
    return run_bass_kernel_spmd(nc, maps, core_ids=list(range(8))).results


def kernel(**inp):
    inp = {k_: np.asarray(v) for k_, v in inp.items()}
    gcol = host_common(inp)
    rA = run(get_nc("A", build_A), host_A(inp))
    hT = [r["hT_out"] for r in rA]
    latT = np.ascontiguousarray(np.concatenate([r["latT"] for r in rA], axis=1))
    rB = run(get_nc("B", build_B), host_B(inp, latT))
    oT_full = np.concatenate([r["oT"] for r in rB], axis=0)
    mapsC = [dict(hT_in=hT[c], gcol=gcol, oT=np.ascontiguousarray(oT_full[:, c * 2048:(c + 1) * 2048]),
                  c_w_o=inp["c_w_o"][0], w_up=inp["w_up"][2], w_down=inp["w_down"][2]) for c in range(8)]
    rC = run(get_nc("C", build_C), mapsC)
    hT = [r["hT_out"] for r in rC]
    z3T = np.concatenate([r["zT_out"] for r in rC], axis=1)
    rD = run(get_nc("D", build_D), host_D(inp, z3T))
    yT_full = np.concatenate([r["yT"] for r in rD], axis=0)
    mapsE = [dict(hT_in=hT[c], gcol=gcol, yT=np.ascontiguousarray(yT_full[:, c * 2048:(c + 1) * 2048]),
                  d_w_glu_a=inp["d_w_glu_a"][0], d_w_glu_b=inp["d_w_glu_b"][0],
                  w_up=inp["w_up"][3], w_down=inp["w_down"][3]) for c in range(8)]
    rE = run(get_nc("E", build_E), mapsE)
    out = np.concatenate([r["hT_out"] for r in rE], axis=1)
    return np.ascontiguousarray(out.T)[None].astype(np.float32)
```
